# Optimizing a Trainium2 kernel written in Bass

```python
import jax, jax.numpy as jnp
from jax import lax
import numpy as np

D_MODEL = 1024
BATCH = 4
SEQ = 4096
DEPTH = 2

N_A_LAYERS = DEPTH // 2
N_B_LAYERS = DEPTH - N_A_LAYERS
D_FF = 2816
EPS = 1e-6
NEG = -1e30

M_HEADS = 8
M_DV = D_MODEL // M_HEADS
M_DQK = M_DV // 2
M_CHUNK = 64

N_QH = 16
N_KVH = 4
HEAD_DIM = 64
CMP_BLOCK = 32
CMP_STRIDE = 16
CMP_HIDDEN = 256
SEL_BLOCK = 64
SEL_TOPK = 16
WINDOW = 512
Q_BLOCK = 64
FORCE_SCORE = 1e4

kernel_name = "yoco_mlstm_nsa_macaron"


def rmsnorm(x, g):
    xf = x.astype(jnp.float32)
    y = xf * lax.rsqrt(jnp.mean(xf * xf, axis=-1, keepdims=True) + EPS)
    return (y * g.astype(jnp.float32)).astype(x.dtype)


def swiglu(h, w_in, w_out):
    a, b = jnp.split(h @ w_in, 2, axis=-1)
    return (jax.nn.silu(a) * b) @ w_out


def masked_softmax(s, mask):
    s = jnp.where(mask, s.astype(jnp.float32), NEG)
    e = jnp.where(mask, jnp.exp(s - jnp.max(s, axis=-1, keepdims=True)), 0.0)
    z = jnp.sum(e, axis=-1, keepdims=True)
    return e / jnp.where(z > 0, z, 1.0)


def mlstm_mixer(h, w_in, b_if, g_head, w_out):
    B, T, _ = h.shape
    H, L = M_HEADS, M_CHUNK
    NC = T // L
    qk_w, v_w = H * M_DQK, H * M_DV
    proj = h @ w_in
    q, k, v, ig, fg, og = jnp.split(
        proj, [qk_w, 2 * qk_w, 2 * qk_w + v_w, 2 * qk_w + v_w + H, 2 * qk_w + v_w + 2 * H], axis=-1)

    def to_chunks(a, d):
        return a.astype(jnp.float32).reshape(B, NC, L, H, d).transpose(0, 3, 1, 2, 4)

    q = to_chunks(q, M_DQK) * (M_DQK ** -0.5)
    k = to_chunks(k, M_DQK)
    v = to_chunks(v, M_DV)
    bif = b_if.astype(jnp.float32)
    log_i = to_chunks(ig, 1)[..., 0] + bif[0][:, None, None]
    log_f = jax.nn.log_sigmoid(to_chunks(fg, 1)[..., 0] + bif[1][:, None, None])

    b = jnp.cumsum(log_f, axis=-1)
    g_tot = b[..., -1]
    a = g_tot[..., None] - b + log_i
    m_loc = jnp.max(a, axis=-1)
    w_loc = jnp.exp(a - m_loc[..., None])
    C_loc = jnp.einsum('bhcl,bhclk,bhclv->bhckv', w_loc, k, v)
    n_loc = jnp.einsum('bhcl,bhclk->bhck', w_loc, k)

    def step(carry, inp):
        C, n, m = carry
        g, Cl, nl, ml = inp
        m_new = jnp.maximum(g + m, ml)
        s_old = jnp.exp(g + m - m_new)
        s_new = jnp.exp(ml - m_new)
        C_new = s_old[..., None, None] * C + s_new[..., None, None] * Cl
        n_new = s_old[..., None] * n + s_new[..., None] * nl
        return (C_new, n_new, m_new), (C, n, m)

    init = (jnp.zeros((B, H, M_DQK, M_DV), jnp.float32),
            jnp.zeros((B, H, M_DQK), jnp.float32),
            jnp.zeros((B, H), jnp.float32))
    xs = (jnp.moveaxis(g_tot, 2, 0), jnp.moveaxis(C_loc, 2, 0),
          jnp.moveaxis(n_loc, 2, 0), jnp.moveaxis(m_loc, 2, 0))
    _, (C_prev, n_prev, m_prev) = lax.scan(step, init, xs)
    C_prev = jnp.moveaxis(C_prev, 0, 2)
    n_prev = jnp.moveaxis(n_prev, 0, 2)
    m_prev = jnp.moveaxis(m_prev, 0, 2)

    causal = jnp.tril(jnp.ones((L, L), bool))
    D = jnp.where(causal, b[..., :, None] - b[..., None, :] + log_i[..., None, :], -jnp.inf)
    inter = b + m_prev[..., None]
    m_t = jnp.maximum(inter, jnp.max(D, axis=-1))
    W = jnp.exp(D - m_t[..., None]) * jnp.einsum('bhctk,bhcsk->bhcts', q, k)
    s_inter = jnp.exp(inter - m_t)
    num = (s_inter[..., None] * jnp.einsum('bhctk,bhckv->bhctv', q, C_prev)
           + jnp.einsum('bhcts,bhcsv->bhctv', W, v))
    den = s_inter * jnp.einsum('bhctk,bhck->bhct', q, n_prev) + jnp.sum(W, axis=-1)
    h_out = num / jnp.maximum(jnp.abs(den), jnp.exp(-m_t))[..., None]
    h_out = h_out.astype(h.dtype).transpose(0, 2, 3, 1, 4).reshape(B, T, H, M_DV)
    h_out = rmsnorm(h_out, g_head).reshape(B, T, H * M_DV)
    return (h_out * jax.nn.sigmoid(og)) @ w_out


def nsa_shared_kv(s, kv_norm, kv_w, cmp_pe, cmp_w1, cmp_w2, k_norm):
    B, T, _ = s.shape
    n_cmp = (T - CMP_BLOCK) // CMP_STRIDE + 1
    kv = rmsnorm(s, kv_norm) @ kv_w
    kc, vc, ks, vs, kw, vw = [a.reshape(B, T, N_KVH, HEAD_DIM) for a in jnp.split(kv, 6, axis=-1)]
    idx = jnp.arange(n_cmp)[:, None] * CMP_STRIDE + jnp.arange(CMP_BLOCK)[None, :]

    def compress(a, pe, w1, w2):
        blk = a[:, idx] + pe[:, None, :]
        flat = blk.transpose(0, 1, 3, 2, 4).reshape(B, n_cmp, N_KVH, CMP_BLOCK * HEAD_DIM)
        return jax.nn.silu(flat @ w1) @ w2

    k_cmp = rmsnorm(compress(kc, cmp_pe[0], cmp_w1[0], cmp_w2[0]), k_norm[0])
    v_cmp = compress(vc, cmp_pe[1], cmp_w1[1], cmp_w2[1])
    ks = rmsnorm(ks, k_norm[1])
    kw = rmsnorm(kw, k_norm[2])
    return (k_cmp, v_cmp, ks, vs, kw, vw)


def cmp_sel_overlap(n_cmp, n_sel):
    c0 = jnp.arange(n_cmp) * CMP_STRIDE
    s0 = jnp.arange(n_sel) * SEL_BLOCK
    ov = (c0[:, None] < s0[None, :] + SEL_BLOCK) & (c0[:, None] + CMP_BLOCK > s0[None, :])
    return ov.astype(jnp.float32)


def nsa_mixer(h, kv, w_in, q_norm, w_out):
    k_cmp, v_cmp, k_sel, v_sel, k_win, v_win = kv
    B, T, _ = h.shape
    G = N_QH // N_KVH
    n_cmp = k_cmp.shape[1]
    n_sel = T // SEL_BLOCK
    top = min(SEL_TOPK, n_sel)
    proj = h @ w_in
    q = proj[..., :N_QH * HEAD_DIM].reshape(B, T, N_KVH, G, HEAD_DIM)
    q = rmsnorm(q, q_norm) * (HEAD_DIM ** -0.5)
    gates = jax.nn.sigmoid(proj[..., N_QH * HEAD_DIM:].astype(jnp.float32)).reshape(B, T, 3, N_KVH, G)

    overlap = cmp_sel_overlap(n_cmp, n_sel)
    cmp_end = jnp.arange(n_cmp) * CMP_STRIDE + CMP_BLOCK - 1
    sel_id = jnp.arange(n_sel)
    ks_blk = k_sel.reshape(B, n_sel, SEL_BLOCK, N_KVH, HEAD_DIM).transpose(0, 3, 1, 2, 4)
    vs_blk = v_sel.reshape(B, n_sel, SEL_BLOCK, N_KVH, HEAD_DIM).transpose(0, 3, 1, 2, 4)
    kw_pad = jnp.pad(k_win, ((0, 0), (WINDOW, 0), (0, 0), (0, 0)))
    vw_pad = jnp.pad(v_win, ((0, 0), (WINDOW, 0), (0, 0), (0, 0)))
    b_ix = jnp.arange(B)[:, None, None, None]
    h_ix = jnp.arange(N_KVH)[None, None, :, None]
    dt = v_sel.dtype

    def block(qb):
        s0 = qb * Q_BLOCK
        t = s0 + jnp.arange(Q_BLOCK)
        qq = lax.dynamic_slice_in_dim(q, s0, Q_BLOCK, axis=1)
        gb = lax.dynamic_slice_in_dim(gates, s0, Q_BLOCK, axis=1).astype(dt)
        s_c = jnp.einsum('bqhgd,bchd->bqhgc', qq, k_cmp)
        p_c = masked_softmax(s_c, (cmp_end[None, :] <= t[:, None])[None, :, None, None, :])
        o_c = jnp.einsum('bqhgc,bchd->bqhgd', p_c.astype(dt), v_cmp)
        imp = jnp.einsum('bqhgc,cj->bqhj', p_c, overlap)
        valid = (sel_id[None, :] * SEL_BLOCK <= t[:, None])
        cur = (t // SEL_BLOCK)[:, None]
        forced = (sel_id[None, :] == 0) | (sel_id[None, :] == cur) | (sel_id[None, :] == cur - 1)
        score = jnp.where((forced & valid)[None, :, None, :], FORCE_SCORE,
                          jnp.where(valid[None, :, None, :], imp, -1.0))
        val, idx = lax.top_k(score, top)
        k_g = ks_blk[b_ix, h_ix, idx]
        v_g = vs_blk[b_ix, h_ix, idx]
        pos = idx[..., None] * SEL_BLOCK + jnp.arange(SEL_BLOCK)
        m_s = (pos <= t[None, :, None, None, None]) & (val >= 0)[..., None]
        s_s = jnp.einsum('bqhgd,bqhksd->bqhgks', qq, k_g).reshape(B, Q_BLOCK, N_KVH, G, top * SEL_BLOCK)
        p_s = masked_softmax(s_s, m_s.reshape(B, Q_BLOCK, N_KVH, 1, top * SEL_BLOCK))
        o_s = jnp.einsum('bqhgn,bqhnd->bqhgd', p_s.astype(dt),
                         v_g.reshape(B, Q_BLOCK, N_KVH, top * SEL_BLOCK, HEAD_DIM))
        k_w = lax.dynamic_slice_in_dim(kw_pad, s0, WINDOW + Q_BLOCK, axis=1)
        v_w = lax.dynamic_slice_in_dim(vw_pad, s0, WINDOW + Q_BLOCK, axis=1)
        kpos = s0 - WINDOW + jnp.arange(WINDOW + Q_BLOCK)
        m_w = (kpos[None, :] <= t[:, None]) & (kpos[None, :] > t[:, None] - WINDOW) & (kpos[None, :] >= 0)
        s_w = jnp.einsum('bqhgd,bkhd->bqhgk', qq, k_w)
        p_w = masked_softmax(s_w, m_w[None, :, None, None, :])
        o_w = jnp.einsum('bqhgk,bkhd->bqhgd', p_w.astype(dt), v_w)
        return (gb[:, :, 0, :, :, None] * o_c + gb[:, :, 1, :, :, None] * o_s
                + gb[:, :, 2, :, :, None] * o_w)

    o = lax.map(block, jnp.arange(T // Q_BLOCK))
    o = jnp.moveaxis(o, 0, 1).reshape(B, T, N_QH * HEAD_DIM)
    return o @ w_out


def setup_inputs(seed: int = 0) -> dict:
    key = jax.random.key(seed)
    k = jax.random.split(key, 20)
    f32 = jnp.float32

    def w(kk, shape, fan_in):
        return jax.random.normal(kk, shape, f32) * fan_in ** -0.5

    def gain(kk, shape):
        return 1.0 + 0.02 * jax.random.normal(kk, shape, f32)

    a_in = 2 * M_HEADS * M_DQK + M_HEADS * M_DV + 2 * M_HEADS + D_MODEL
    b_in = N_QH * HEAD_DIM + 3 * N_QH
    nb = jax.random.normal(k[6], (N_A_LAYERS, 2, M_HEADS), f32)
    return {
        "x": jax.random.normal(k[0], (BATCH, SEQ, D_MODEL), f32),
        "ffn_norm": gain(k[1], (DEPTH, 2, D_MODEL)),
        "ffn_w_in": w(k[2], (DEPTH, 2, D_MODEL, 2 * D_FF), D_MODEL),
        "ffn_w_out": w(k[3], (DEPTH, 2, D_FF, D_MODEL), D_FF),
        "mix_norm": gain(k[4], (DEPTH, D_MODEL)),
        "a_w_in": w(k[5], (N_A_LAYERS, D_MODEL, a_in), D_MODEL),
        "a_b_if": jnp.stack([-1.0 + 0.1 * nb[:, 0], 3.0 + 0.5 * nb[:, 1]], axis=1),
        "a_g_head": gain(k[7], (N_A_LAYERS, M_HEADS, M_DV)),
        "a_w_out": w(k[8], (N_A_LAYERS, M_HEADS * M_DV, D_MODEL), M_HEADS * M_DV),
        "kv_norm": gain(k[9], (D_MODEL,)),
        "kv_w": w(k[10], (D_MODEL, 6 * N_KVH * HEAD_DIM), D_MODEL),
        "cmp_pe": 0.02 * jax.random.normal(k[11], (2, CMP_BLOCK, HEAD_DIM), f32),
        "cmp_w1": w(k[12], (2, CMP_BLOCK * HEAD_DIM, CMP_HIDDEN), CMP_BLOCK * HEAD_DIM),
        "cmp_w2": w(k[13], (2, CMP_HIDDEN, HEAD_DIM), CMP_HIDDEN),
        "k_norm": gain(k[14], (3, HEAD_DIM)),
        "b_w_in": w(k[15], (N_B_LAYERS, D_MODEL, b_in), D_MODEL),
        "b_q_norm": gain(k[16], (N_B_LAYERS, HEAD_DIM)),
        "b_w_out": w(k[17], (N_B_LAYERS, N_QH * HEAD_DIM, D_MODEL), N_QH * HEAD_DIM),
    }


def reference(x, ffn_norm, ffn_w_in, ffn_w_out, mix_norm, a_w_in, a_b_if, a_g_head, a_w_out,
              kv_norm, kv_w, cmp_pe, cmp_w1, cmp_w2, k_norm, b_w_in, b_q_norm, b_w_out):
    h = x
    kv = None
    for layer in range(DEPTH):
        h = h + 0.5 * swiglu(rmsnorm(h, ffn_norm[layer, 0]), ffn_w_in[layer, 0], ffn_w_out[layer, 0])
        hn = rmsnorm(h, mix_norm[layer])
        if layer < N_A_LAYERS:
            h = h + mlstm_mixer(hn, a_w_in[layer], a_b_if[layer], a_g_head[layer], a_w_out[layer])
        else:
            j = layer - N_A_LAYERS
            h = h + nsa_mixer(hn, kv, b_w_in[j], b_q_norm[j], b_w_out[j])
        h = h + 0.5 * swiglu(rmsnorm(h, ffn_norm[layer, 1]), ffn_w_in[layer, 1], ffn_w_out[layer, 1])
        if layer == N_A_LAYERS - 1:
            kv = nsa_shared_kv(h, kv_norm, kv_w, cmp_pe, cmp_w1, cmp_w2, k_norm)
    return h
```

```python
import contextlib
import numpy as np
import concourse.bass as bass
import concourse.mybir as mybir
from concourse.bass_utils import run_bass_kernel_spmd

F32 = mybir.dt.float32
BF16 = mybir.dt.bfloat16
AF = mybir.ActivationFunctionType
ALU = mybir.AluOpType
AX = mybir.AxisListType

D_MODEL = 1024
D_FF = 2816
EPS = 1e-6
NEGM = -30000.0


class Buf:
    __slots__ = ("name", "writer", "readers")

    def __init__(self, name=""):
        self.name = name
        self.writer = None
        self.readers = []


class Sched:
    COMPUTE = ("pe", "act", "dve", "pool")
    ENGS = ("pe", "act", "dve", "pool", "sp")

    def __init__(self, nc, n_dma_sems=16, same_engine_sync=True):
        self.nc = nc
        self.same_engine_sync = same_engine_sync
        self.lists = {k: [] for k in self.ENGS}
        self.sems = {}
        self.cnt = {}
        self.seen = {k: {} for k in self.ENGS}
        self._ctx = []
        for k in self.COMPUTE:
            self._mksem("pc_" + k)
        self.dma_sems = {}
        for q in ("sp", "pool", "act", "cc"):
            self.dma_sems[q] = []
            for i in range(n_dma_sems if q != "cc" else 2):
                key = "dma_%s_%d" % (q, i)
                self._mksem(key)
                self.dma_sems[q].append(key)
        self.dma_rr = {"sp": 0, "pool": 0, "act": 0, "cc": 0}

    def _mksem(self, key):
        cm = self.nc.semaphore(key)
        h = cm.__enter__()
        self._ctx.append(cm)
        self.sems[key] = h
        self.cnt[key] = 0

    def _need(self, eng, deps, ev):
        if ev is None:
            return
        key, val = ev
        if key == "pc_pe" and eng == "pe":
            return
        if (not self.same_engine_sync) and key == "pc_" + eng:
            return
        if self.seen[eng].get(key, 0) >= val:
            return
        if deps.get(key, 0) < val:
            deps[key] = val

    def _collect(self, eng, reads, writes):
        deps = {}
        for b in reads:
            self._need(eng, deps, b.writer)
        for b in writes:
            self._need(eng, deps, b.writer)
            for r in b.readers:
                self._need(eng, deps, r)
        return deps

    def _commit(self, ev, reads, writes):
        for b in reads:
            b.readers.append(ev)
        for b in writes:
            b.writer = ev
            b.readers = []

    def op(self, eng, fn, reads=(), writes=()):
        deps = self._collect(eng, reads, writes)
        for k, v in deps.items():
            self.seen[eng][k] = v
        key = "pc_" + eng
        self.cnt[key] += 1
        ev = (key, self.cnt[key])
        self.lists[eng].append((list(deps.items()), fn, (key, 1)))
        self._commit(ev, reads, writes)
        return ev

    def dma(self, q, fn, reads=(), writes=()):
        eng = "pool" if q == "cc" else q
        deps = self._collect(eng, reads, writes)
        sems = self.dma_sems[q]
        key = sems[self.dma_rr[q] % len(sems)]
        self.dma_rr[q] += 1
        if self.cnt[key] > 0 and self.seen[eng].get(key, 0) < self.cnt[key]:
            deps[key] = max(deps.get(key, 0), self.cnt[key])
        for k, v in deps.items():
            self.seen[eng][k] = v
        self.cnt[key] += 16
        ev = (key, self.cnt[key])
        self.lists[eng].append((list(deps.items()), fn, (key, 16)))
        self._commit(ev, reads, writes)
        return ev

    def barrier(self):
        for eng in self.ENGS:
            deps = {}
            for key, val in self.cnt.items():
                if val == 0:
                    continue
                if key == "pc_" + eng and eng == "pe":
                    continue
                if self.seen[eng].get(key, 0) < val:
                    deps[key] = val
            for k, v in deps.items():
                self.seen[eng][k] = v
            if deps:
                self.lists[eng].append((list(deps.items()), None, None))

    def emit(self):
        nc = self.nc
        sems = self.sems
        lists = self.lists
        with nc.Block() as block:
            def run(engname):
                def body(e):
                    for waits, fn, inc in lists[engname]:
                        for k, v in waits:
                            e.wait_ge(sems[k], v)
                        if fn is not None:
                            ins = fn(e)
                            ins.then_inc(sems[inc[0]], inc[1])
                return body
            block.tensor(run("pe"))
            block.scalar(run("act"))
            block.vector(run("dve"))
            block.gpsimd(run("pool"))
            block.sync(run("sp"))
        self.lists = {k: [] for k in self.ENGS}

    def close(self):
        for cm in reversed(self._ctx):
            cm.__exit__(None, None, None)
        self._ctx = []


class T:
    def __init__(self, t, b):
        self.t = t
        self.b = b

    def __getitem__(self, k):
        return self.t[k]


_uid = [0]


def _nm(name):
    _uid[0] += 1
    return "%s_%d" % (name, _uid[0])


class KB:
    def __init__(self, nc, **kw):
        self.nc = nc
        self.S = Sched(nc, **kw)

    def sb(self, es, name, shape, dt, nbuf=None):
        t = es.enter_context(self.nc.sbuf_tensor(_nm(name), list(shape), dt))
        if nbuf is None:
            return T(t, Buf(name))
        return T(t, [Buf(name + str(i)) for i in range(nbuf)])

    def ps(self, es, name, shape, dt, nbuf=None):
        t = es.enter_context(self.nc.psum_tensor(_nm(name), list(shape), dt))
        if nbuf is None:
            return T(t, Buf(name))
        return T(t, [Buf(name + str(i)) for i in range(nbuf)])


def MM(o, l, r, st, sp):
    return lambda e: e.matmul(o, lhsT=l, rhs=r, start=st, stop=sp)


def TR(o, i, idn):
    return lambda e: e.transpose(out=o, in_=i, identity=idn)


def ACTF(o, i, f, **kw):
    return lambda e: e.activation(out=o, in_=i, func=f, **kw)


def CP(o, i):
    return lambda e: e.tensor_copy(out=o, in_=i)


def ACP(o, i):
    return lambda e: e.copy(out=o, in_=i)


def TS(o, i, s1, s2, op0, op1=None):
    if op1 is None:
        return lambda e: e.tensor_scalar(out=o, in0=i, scalar1=s1, scalar2=None, op0=op0)
    return lambda e: e.tensor_scalar(out=o, in0=i, scalar1=s1, scalar2=s2, op0=op0, op1=op1)


def TT(o, a, b, op):
    return lambda e: e.tensor_tensor(out=o, in0=a, in1=b, op=op)


def STT(o, a, s, b, op0, op1):
    return lambda e: e.scalar_tensor_tensor(out=o, in0=a, scalar=s, in1=b, op0=op0, op1=op1)


def DMA(o, i):
    return lambda e: e.dma_start(out=o, in_=i)


def MSET(o, v):
    return lambda e: e.memset(o, v)


def load_consts(K, es, ident_ap):
    S = K.S
    K.ident = K.sb(es, "ident", [128, 128], BF16)
    K.identf = K.sb(es, "identf", [128, 128], F32)
    S.dma("pool", DMA(K.ident[:], ident_ap), writes=[K.ident.b])
    S.dma("sp", DMA(K.identf[:], ident_ap), writes=[K.identf.b])
    K.epsc = K.sb(es, "epsc", [128, 1], F32)
    S.op("pool", MSET(K.epsc[:], EPS), writes=[K.epsc.b])


def rms_rstd(K, ss_ap, rstd_ap, tmp_ap, n, reads, writes):
    S = K.S
    S.op("act", ACTF(tmp_ap, ss_ap, AF.Sqrt, bias=K.epsc[:, 0:1], scale=1.0 / n), reads=list(reads) + [K.epsc.b], writes=writes)
    S.op("dve", lambda e: e.reciprocal(out=rstd_ap, in_=tmp_ap), reads=writes, writes=writes)


def norm_transpose_tile(K, x_ap, x_bufs, gain, wk, hnT_dst, hnT_buf, pT):
    S = K.S
    junk, ss, tmp, rstd, hn = wk["junk"], wk["ss"], wk["tmp"], wk["rstd"], wk["hn"]
    i = wk["ctr"]
    wk["ctr"] += 1
    hb = hn.b[i % 2]
    hnt = hn[:, i % 2, :]
    S.op("dve", MSET(ss[:], 0.0), writes=[ss.b])
    S.op("act", ACTF(junk[:], x_ap, AF.Square, accum_out=ss[:, 0:1]), reads=list(x_bufs) + [ss.b], writes=[ss.b])
    rms_rstd(K, ss[:, 0:1], rstd[:, 0:1], tmp[:, 0:1], D_MODEL, [ss.b], [ss.b])
    S.op("dve", STT(hnt, x_ap, rstd[:, 0:1], gain[:], ALU.mult, ALU.mult),
         reads=list(x_bufs) + [ss.b, gain.b], writes=[hb])
    for k in range(8):
        S.op("pe", TR(pT[:, k, :], hn[:, i % 2, k * 128:(k + 1) * 128], K.ident[:]),
             reads=[hb, K.ident.b], writes=[pT.b])
    S.op("act", ACP(hnT_dst, pT[:]), reads=[pT.b], writes=[hnT_buf])


def ffn_stage(K, src_tiles, src_bufs, dst_tiles, dst_bufs, gain_ap, w_in_ap, w_out_ap, G=2, src2_tiles=None, src2_bufs=None, flag_ap=None):
    nc, S = K.nc, K.S
    NT = len(src_tiles)
    NTG = NT // 4
    NFB = D_FF // 128
    NG = NFB // G
    with contextlib.ExitStack() as es:
        resid = K.sb(es, "resid", [128, NT, 1024], F32, nbuf=NT * 2)
        hnT = K.sb(es, "hnT", [128, 8, NT * 128], BF16, nbuf=NT)
        gain = K.sb(es, "gain", [128, 1024], F32)
        wk = {"junk": K.sb(es, "junk", [128, 1024], F32), "ss": K.sb(es, "ss", [128, 1], F32),
              "tmp": K.sb(es, "tmp", [128, 1], F32), "rstd": K.sb(es, "rstd", [128, 1], F32),
              "hn": K.sb(es, "hn", [128, 2, 1024], BF16, nbuf=2), "ctr": 0}
        wi = K.sb(es, "wi", [128, 2, 8, 2, G * 128], BF16, nbuf=4)
        wo = K.sb(es, "wo", [128, 2, G, 1024], BF16, nbuf=2)
        gT = K.sb(es, "gT", [128, 2, G, NT * 128], BF16, nbuf=2 * G * NTG)
        sa = K.sb(es, "sa", [128, 2, 512], F32, nbuf=2)
        pT = K.ps(es, "pT", [128, 8, 128], BF16)
        pa = K.ps(es, "pa", [128, 2, 512], F32, nbuf=2)
        pb = K.ps(es, "pb", [128, 2, 512], F32, nbuf=2)
        po = K.ps(es, "po", [128, 2, 512], F32, nbuf=2)

        S.dma("sp", DMA(gain[:], gain_ap.partition_broadcast(128)), writes=[gain.b])
        w_in_v = w_in_ap.rearrange("(k p) f -> p k f", p=128)
        w_out_v = w_out_ap.rearrange("(f p) n -> p f n", p=128)

        def load_w(g):
            gb = g % 2
            for ab in range(2):
                c0 = ab * D_FF + g * G * 128
                S.dma("pool", DMA(wi[:, gb, :, ab, :], w_in_v[:, :, c0:c0 + G * 128]), writes=[wi.b[gb * 2 + ab]])
            S.dma("pool", DMA(wo[:, gb, :, :], w_out_v[:, g * G:(g + 1) * G, :]), writes=[wo.b[gb]])

        load_w(0)
        if src2_tiles is not None:
            flg = K.sb(es, "flg", [128, 2], F32)
            alt = K.sb(es, "alt", [128, 2, 1024], F32, nbuf=2)
            S.dma("sp", DMA(flg[:], flag_ap), writes=[flg.b])
        for i in range(NT):
            S.dma("sp", DMA(resid[:, i, :], src_tiles[i]), reads=[src_bufs[i]],
                  writes=[resid.b[2 * i], resid.b[2 * i + 1]])
            if src2_tiles is not None:
                ab = i % 2
                rb = [resid.b[2 * i], resid.b[2 * i + 1]]
                S.dma("sp", DMA(alt[:, ab, :], src2_tiles[i]), reads=[src2_bufs[i]], writes=[alt.b[ab]])
                S.op("dve", TS(resid[:, i, :], resid[:, i, :], flg[:, 0:1], None, ALU.mult), reads=rb + [flg.b], writes=rb)
                S.op("dve", STT(resid[:, i, :], alt[:, ab, :], flg[:, 1:2], resid[:, i, :], ALU.mult, ALU.add),
                     reads=rb + [flg.b, alt.b[ab]], writes=rb)
        for i in range(NT):
            norm_transpose_tile(K, resid[:, i, :], [resid.b[2 * i], resid.b[2 * i + 1]], gain, wk,
                                hnT[:, :, i * 128:(i + 1) * 128], hnT.b[i], pT)
        u = 0
        ou = 0
        for g in range(NG):
            gb = g % 2
            if g + 1 < NG:
                load_w(g + 1)
            for j in range(G):
                for tg in range(NTG):
                    ub = u % 2
                    u += 1
                    rd = [hnT.b[tg * 4 + q] for q in range(4)]
                    for k in range(8):
                        S.op("pe", MM(pa[:, ub, :], wi[:, gb, k, 0, j * 128:(j + 1) * 128],
                                      hnT[:, k, tg * 512:(tg + 1) * 512], k == 0, k == 7),
                             reads=rd + [wi.b[gb * 2]], writes=[pa.b[ub]])
                    for k in range(8):
                        S.op("pe", MM(pb[:, ub, :], wi[:, gb, k, 1, j * 128:(j + 1) * 128],
                                      hnT[:, k, tg * 512:(tg + 1) * 512], k == 0, k == 7),
                             reads=rd + [wi.b[gb * 2 + 1]], writes=[pb.b[ub]])
                    S.op("act", ACTF(sa[:, ub, :], pa[:, ub, :], AF.Silu), reads=[pa.b[ub]], writes=[sa.b[ub]])
                    gbuf = gT.b[(gb * G + j) * NTG + tg]
                    S.op("dve", TT(gT[:, gb, j, tg * 512:(tg + 1) * 512], sa[:, ub, :], pb[:, ub, :], ALU.mult),
                         reads=[sa.b[ub], pb.b[ub]], writes=[gbuf])
            for tt in range(NT):
                for half in range(2):
                    ob = ou % 2
                    ou += 1
                    for j in range(G):
                        S.op("pe", MM(po[:, ob, :], gT[:, gb, j, tt * 128:(tt + 1) * 128],
                                      wo[:, gb, j, half * 512:(half + 1) * 512], j == 0, j == G - 1),
                             reads=[gT.b[(gb * G + j) * NTG + tt // 4], wo.b[gb]], writes=[po.b[ob]])
                    rb = resid.b[2 * tt + half]
                    rs = resid[:, tt, half * 512:(half + 1) * 512]
                    S.op("dve", STT(rs, po[:, ob, :], 0.5, rs, ALU.mult, ALU.add), reads=[po.b[ob], rb], writes=[rb])
        for i in range(NT):
            S.dma("sp", DMA(dst_tiles[i], resid[:, i, :]), reads=[resid.b[2 * i], resid.b[2 * i + 1]],
                  writes=[dst_bufs[i]])
        S.barrier()
        S.emit()


def bc_last(ap2d, n):
    P, A = ap2d.shape
    return ap2d.unsqueeze(2).to_broadcast([P, A, n])


def bc_mid(ap2d, a):
    P, N = ap2d.shape
    return ap2d.unsqueeze(1).to_broadcast([P, a, N])


def mlstm_stage(K, src_tiles, src_bufs, dst_tiles, dst_bufs, gain_ap, w_in_ap, b_if_ap, g_head_ap, w_out_ap,
                cinit_ap, cfin_ap, cst, mphase="AB", state_only=False, flag_ap=None):
    nc, S = K.nc, K.S
    NT = len(src_tiles)
    with contextlib.ExitStack() as ea:
        win = K.sb(ea, "win", [128, 8, 3088], BF16)
        wout = K.sb(ea, "wout", [128, 8, 1024], BF16)
        gain = K.sb(ea, "gain", [128, 1024], F32)
        ghead = K.sb(ea, "ghead", [128, 1024], F32)
        bif = K.sb(ea, "bif", [128, 16], F32)
        tri = K.sb(ea, "tri", [128, 128], BF16)
        trif = K.sb(ea, "trif", [128, 128], F32)
        onesm = K.sb(ea, "onesm", [128, 128], BF16)
        onec = K.sb(ea, "onec", [128, 16], F32)
        xt = K.sb(ea, "xt", [128, 2, 1024], F32, nbuf=2)
        hnT = K.sb(ea, "hnT", [128, 2, 8, 128], BF16, nbuf=2)
        wk = {"junk": K.sb(ea, "junk", [128, 1024], F32), "ss": K.sb(ea, "ss", [128, 1], F32),
              "tmp": K.sb(ea, "tmp", [128, 1], F32), "rstd": K.sb(ea, "rstd", [128, 1], F32),
              "hn": K.sb(ea, "hn", [128, 2, 1024], BF16, nbuf=2), "ctr": 0}
        qT = K.sb(ea, "qT", [128, 2, 4, 128], BF16, nbuf=2)
        qTz = K.sb(ea, "qTz", [128, 2, 4, 2, 128], BF16, nbuf=2)
        kTz = K.sb(ea, "kTz", [128, 2, 4, 2, 128], BF16, nbuf=2)
        kwz = K.sb(ea, "kwz", [128, 2, 4, 2, 128], BF16, nbuf=2)
        va = K.sb(ea, "va", [128, 2, 8, 130], BF16, nbuf=2)
        sg = K.sb(ea, "sg", [128, 2, 1024], BF16, nbuf=2)
        gt = K.sb(ea, "gt", [128, 16], F32)
        e1 = K.sb(ea, "e1", [128, 8], F32)
        lf = K.sb(ea, "lf", [128, 8], F32)
        lfs = K.sb(ea, "lfs", [128, 2, 8], BF16)
        lfr = K.sb(ea, "lfr", [128, 8], F32)
        t1 = K.sb(ea, "t1", [128, 8], F32)
        t2 = K.sb(ea, "t2", [128, 8], F32)
        ga = K.sb(ea, "ga", [128, 8], F32)
        ebv = K.sb(ea, "ebv", [128, 8], F32)
        wv = K.sb(ea, "wv", [128, 8], F32)
        eGb = K.sb(ea, "eGb", [128, 8], F32)
        eGst = K.sb(ea, "eGst", [128, 4], F32)
        Cst = K.sb(ea, "Cst", [128, 4, 129], F32)
        Cb = K.sb(ea, "Cb", [128, 4, 130], BF16)
        tmpw = K.sb(ea, "tmpw", [128, 4, 128], F32)
        Wt = K.sb(ea, "Wt", [128, 2, 4, 128], BF16, nbuf=2)
        nd = K.sb(ea, "nd", [128, 8, 129], F32)
        rden = K.sb(ea, "rden", [128, 8], F32)
        rden2 = K.sb(ea, "rden2", [128, 8], F32)
        hout = K.sb(ea, "hout", [128, 8, 128], F32)
        sq = K.sb(ea, "sq", [128, 8, 128], F32)
        ssq = K.sb(ea, "ssq", [128, 8], F32)
        rst = K.sb(ea, "rst", [128, 8], F32)
        gsg = K.sb(ea, "gsg", [128, 1024], F32)
        hg = K.sb(ea, "hg", [128, 1024], BF16)
        hgT = K.sb(ea, "hgT", [128, 8, 128], BF16)
        pT = K.ps(ea, "pT", [128, 8, 128], BF16)
        pp = K.ps(ea, "pp", [128, 2, 512], F32, nbuf=2)
        pS = K.ps(ea, "pS", [128, 4, 128], F32)
        pN = K.ps(ea, "pN", [128, 4, 256], F32)
        pC = K.ps(ea, "pC", [128, 4, 256], F32)
        pgb = Buf("pg")
        pg = pC[:, 0, 136:176]
        bank = [0]

        def nb():
            b = bank[0] % 2
            bank[0] += 1
            return b

        w_in_v = w_in_ap.rearrange("(k p) f -> p k f", p=128)
        for c0 in range(0, 3088, 772):
            S.dma("pool", DMA(win[:, :, c0:c0 + 772], w_in_v[:, :, c0:c0 + 772]), writes=[win.b])
        S.dma("pool", DMA(wout[:], w_out_ap.rearrange("(k p) n -> p k n", p=128)), writes=[wout.b])
        S.dma("pool", DMA(tri[:], cst["tri"]), writes=[tri.b])
        S.dma("pool", DMA(onesm[:], cst["ones128"]), writes=[onesm.b])
        S.dma("sp", DMA(trif[:], cst["tri"]), writes=[trif.b])
        S.dma("sp", DMA(gain[:], gain_ap.partition_broadcast(128)), writes=[gain.b])
        S.dma("sp", DMA(ghead[:], g_head_ap.rearrange("a b -> (a b)").partition_broadcast(128)), writes=[ghead.b])
        S.dma("sp", DMA(bif[:], b_if_ap.rearrange("a b -> (a b)").partition_broadcast(128)), writes=[bif.b])
        if cinit_ap is None:
            S.op("dve", MSET(Cst[:], 0.0), writes=[Cst.b])
        else:
            S.dma("sp", DMA(Cst[:], cinit_ap), writes=[Cst.b])
            if flag_ap is not None:
                flg = K.sb(ea, "flg", [128, 2], F32)
                S.dma("sp", DMA(flg[:], flag_ap), writes=[flg.b])
                S.op("dve", TS(Cst[:], Cst[:], flg[:, 1:2], None, ALU.mult), reads=[Cst.b, flg.b], writes=[Cst.b])
        S.op("dve", MSET(onec[:], 1.0), writes=[onec.b])
        S.op("dve", MSET(Cb[:], 0.0), writes=[Cb.b])
        S.op("dve", CP(Cb[:, :, 0:129], Cst[:]), reads=[Cst.b], writes=[Cb.b])
        for t_ in (qTz, kTz, kwz):
            S.op("pool", MSET(t_[:].rearrange("p a m r t -> p (a m r t)"), 0.0), writes=list(t_.b))
        S.op("dve", CP(va[:, :, :, 128:129], onec[:].rearrange("p (a h o) -> p a h o", a=2, o=1)),
             reads=[onec.b], writes=list(va.b))
        for i in range(NT):
            xb = i % 2
            S.dma("sp", DMA(xt[:, xb, :], src_tiles[i]), reads=[src_bufs[i]], writes=[xt.b[xb]])
            norm_transpose_tile(K, xt[:, xb, :], [xt.b[xb]], gain, wk, hnT[:, xb, :, :], hnT.b[xb], pT)
            hb = hnT.b[xb]
            for which in range(0 if not state_only else 2, 2):
                b = nb()
                for m in range(4):
                    c0 = which * 512 + m * 128
                    for k in range(8):
                        S.op("pe", MM(pp[:, b, m * 128:(m + 1) * 128], win[:, k, c0:c0 + 128], hnT[:, xb, k, :],
                                      k == 0, k == 7), reads=[win.b, hb], writes=[pp.b[b]])
                src = pp[:, b, :].rearrange("p (m t) -> p m t", m=4)
                if which == 0:
                    S.op("act", lambda e, o=qT[:, xb, :, :], s_=src: e.mul(out=o, in_=s_, mul=0.125),
                         reads=[pp.b[b]], writes=[qT.b[xb]])
                    for r in range(2):
                        S.op("act", lambda e, o=qTz[r * 64:(r + 1) * 64, xb, :, r, :], s_=src[r * 64:(r + 1) * 64]:
                             e.mul(out=o, in_=s_, mul=0.125), reads=[pp.b[b]], writes=[qTz.b[xb]])
                else:
                    for r in range(2):
                        S.op("act", ACP(kTz[r * 64:(r + 1) * 64, xb, :, r, :], src[r * 64:(r + 1) * 64]),
                             reads=[pp.b[b]], writes=[kTz.b[xb]])
            for k in range(8):
                S.op("pe", MM(pg[:, 0:16], hnT[:, xb, k, :], win[:, k, 2048:2064], k == 0, k == 7),
                     reads=[win.b, hb], writes=[pgb])
            S.op("dve", TT(gt[:], pg[:, 0:16], bif[:], ALU.add), reads=[pgb, bif.b], writes=[gt.b])
            S.op("act", ACTF(e1[:], gt[:, 8:16], AF.Exp, scale=-1.0), reads=[gt.b], writes=[e1.b])
            S.op("act", ACTF(e1[:], e1[:], AF.Ln, bias=onec[:, 0:1]), reads=[e1.b, onec.b], writes=[e1.b])
            S.op("dve", TS(lf[:], e1[:], -1.0, None, ALU.mult), reads=[e1.b], writes=[lf.b])
            S.op("dve", CP(lfs[:, 0, :], lf[:]), reads=[lf.b], writes=[lfs.b])
            S.op("dve", TT(lfr[:], lf[:], lfs[:, 0, :], ALU.subtract), reads=[lf.b, lfs.b], writes=[lfr.b])
            S.op("dve", CP(lfs[:, 1, :], lfr[:]), reads=[lfr.b], writes=[lfs.b])
            for j in range(2):
                S.op("pe", MM(pg[:, 16:24], tri[:], lfs[:, j, :], j == 0, j == 1), reads=[tri.b, lfs.b], writes=[pgb])
            for j in range(2):
                S.op("pe", MM(pg[:, 24:32], onesm[:], lfs[:, j, :], j == 0, j == 1), reads=[onesm.b, lfs.b], writes=[pgb])
            S.op("dve", TT(t1[:], gt[:, 0:8], pg[:, 16:24], ALU.subtract), reads=[gt.b, pgb], writes=[t1.b])
            S.op("dve", TT(t2[:], t1[:], pg[:, 24:32], ALU.add), reads=[t1.b, pgb], writes=[t2.b])
            S.op("act", ACTF(ga[:], t1[:], AF.Exp), reads=[t1.b], writes=[ga.b])
            S.op("act", ACTF(ebv[:], pg[:, 16:24], AF.Exp), reads=[pgb], writes=[ebv.b])
            S.op("act", ACTF(wv[:], t2[:], AF.Exp), reads=[t2.b], writes=[wv.b])
            S.op("act", ACTF(eGb[:], pg[:, 24:32], AF.Exp), reads=[pgb], writes=[eGb.b])
            for r in range(2):
                S.op("dve", CP(eGst[r * 64:(r + 1) * 64, :], eGb[r * 64:(r + 1) * 64, :].rearrange("p (m r) -> p m r", r=2)[:, :, r]),
                     reads=[eGb.b], writes=[eGst.b])
            bk = nb()
            for k in range(8):
                S.op("pe", MM(pp[:, bk, :], hnT[:, xb, k, :], win[:, k, 512:1024], k == 0, k == 7),
                     reads=[win.b, hb], writes=[pp.b[bk]])
            for r in range(2):
                S.op("dve", TT(kwz[:, xb, :, r, r * 64:(r + 1) * 64],
                               pp[:, bk, :].rearrange("p (m r d) -> p m r d", m=4, r=2)[:, :, r, :],
                               bc_last(wv[:].rearrange("p (m r) -> p m r", r=2)[:, :, r], 64), ALU.mult),
                     reads=[pp.b[bk], wv.b], writes=[kwz.b[xb]])
            for half in range(2):
                b = nb()
                c0 = 1024 + half * 512
                for k in range(8):
                    S.op("pe", MM(pp[:, b, :], hnT[:, xb, k, :], win[:, k, c0:c0 + 512], k == 0, k == 7),
                         reads=[win.b, hb], writes=[pp.b[b]])
                S.op("act", ACP(va[:, xb, half * 4:(half + 1) * 4, 0:128],
                                pp[:, b, :].rearrange("p (h v) -> p h v", h=4)),
                     reads=[pp.b[b]], writes=[va.b[xb]])
            for half in range(0 if not state_only else 2, 2):
                b = nb()
                c0 = 2064 + half * 512
                for k in range(8):
                    S.op("pe", MM(pp[:, b, :], hnT[:, xb, k, :], win[:, k, c0:c0 + 512], k == 0, k == 7),
                         reads=[win.b, hb], writes=[pp.b[b]])
                S.op("act", ACTF(sg[:, xb, half * 512:(half + 1) * 512], pp[:, b, :], AF.Sigmoid),
                     reads=[pp.b[b]], writes=[sg.b[xb]])
            for hh in range(0 if not state_only else 2, 2):
                wb_ = hh
                for hl in range(4):
                    h = hh * 4 + hl
                    m, r = h // 2, h % 2
                    S.op("pe", MM(pS[:, hl, :], kTz[:, xb, m, r, :], qT[:, xb, m, :], True, True),
                         reads=[kTz.b[xb], qT.b[xb]], writes=[pS.b])
                S.op("dve", TT(tmpw[:], pS[:], bc_last(ga[:, hh * 4:(hh + 1) * 4], 128), ALU.mult),
                     reads=[pS.b, ga.b], writes=[tmpw.b])
                S.op("dve", TT(Wt[:, wb_, :, :], tmpw[:], bc_mid(trif[:], 4), ALU.mult),
                     reads=[tmpw.b, trif.b], writes=[Wt.b[wb_]])
                for hl in range(4):
                    h = hh * 4 + hl
                    m, r = h // 2, h % 2
                    S.op("pe", MM(pN[:, hl, 0:129], Wt[:, wb_, hl, :], va[:, xb, h, 0:129], True, False),
                         reads=[Wt.b[wb_], va.b[xb]], writes=[pN.b])
                    S.op("pe", MM(pN[:, hl, 0:129], qTz[:, xb, m, r, :], Cb[:, m, 0:129], False, True),
                         reads=[qTz.b[xb], Cb.b], writes=[pN.b])
                S.op("dve", TT(nd[:, hh * 4:(hh + 1) * 4, :], pN[:, :, 0:129], bc_last(ebv[:, hh * 4:(hh + 1) * 4], 129),
                               ALU.mult), reads=[pN.b, ebv.b], writes=[nd.b])
            for m in range(4):
                for r in range(2):
                    S.op("pe", MM(pC[:, m, 0:129], kwz[:, xb, m, r, :], va[:, xb, 2 * m + r, 0:129], r == 0, r == 1),
                         reads=[kwz.b[xb], va.b[xb]], writes=[pC.b])
            for m in range(4):
                S.op("dve", STT(Cst[:, m, :], Cst[:, m, :], eGst[:, m:m + 1], pC[:, m, 0:129], ALU.mult, ALU.add),
                     reads=[Cst.b, eGst.b, pC.b], writes=[Cst.b])
            if state_only:
                continue
            S.op("dve", CP(Cb[:, :, 0:129], Cst[:]), reads=[Cst.b], writes=[Cb.b])
            S.op("dve", TS(rden[:], nd[:, :, 128], -1.0, None, ALU.mult), reads=[nd.b], writes=[rden.b])
            S.op("dve", TT(rden2[:], rden[:], nd[:, :, 128], ALU.max), reads=[nd.b, rden.b], writes=[rden2.b])
            S.op("dve", TS(rden[:], rden2[:], 1.0, None, ALU.max), reads=[rden2.b], writes=[rden.b])
            S.op("dve", lambda e: e.reciprocal(out=rden2[:], in_=rden[:]), reads=[rden.b], writes=[rden2.b])
            S.op("dve", TT(hout[:], nd[:, :, 0:128], bc_last(rden2[:], 128), ALU.mult),
                 reads=[nd.b, rden2.b], writes=[hout.b])
            S.op("pool", TT(sq[:], hout[:], hout[:], ALU.mult), reads=[hout.b], writes=[sq.b])
            S.op("dve", lambda e: e.tensor_reduce(out=ssq[:], in_=sq[:], axis=AX.X, op=ALU.add),
                 reads=[sq.b], writes=[ssq.b])
            S.op("act", ACTF(rst[:], ssq[:], AF.Sqrt, bias=K.epsc[:, 0:1], scale=1.0 / 128),
                 reads=[ssq.b, K.epsc.b], writes=[rst.b])
            S.op("dve", lambda e: e.reciprocal(out=ssq[:], in_=rst[:]), reads=[rst.b], writes=[ssq.b])
            S.op("pool", TT(gsg[:], ghead[:], sg[:, xb, :], ALU.mult), reads=[ghead.b, sg.b[xb]], writes=[gsg.b])
            S.op("dve", TT(sq[:], hout[:], bc_last(ssq[:], 128), ALU.mult), reads=[hout.b, ssq.b], writes=[sq.b])
            S.op("dve", TT(hg[:], sq[:].rearrange("p h v -> p (h v)"), gsg[:], ALU.mult),
                 reads=[sq.b, gsg.b], writes=[hg.b])
            for k in range(8):
                S.op("pe", TR(pT[:, k, :], hg[:, k * 128:(k + 1) * 128], K.ident[:]),
                     reads=[hg.b, K.ident.b], writes=[pT.b])
            S.op("act", ACP(hgT[:], pT[:]), reads=[pT.b], writes=[hgT.b])
            for half in range(2):
                b = nb()
                for k in range(8):
                    S.op("pe", MM(pp[:, b, :], hgT[:, k, :], wout[:, k, half * 512:(half + 1) * 512], k == 0, k == 7),
                         reads=[hgT.b, wout.b], writes=[pp.b[b]])
                S.op("dve", TT(xt[:, xb, half * 512:(half + 1) * 512], xt[:, xb, half * 512:(half + 1) * 512], pp[:, b, :],
                               ALU.add), reads=[xt.b[xb], pp.b[b]], writes=[xt.b[xb]])
            S.dma("sp", DMA(dst_tiles[i], xt[:, xb, :]), reads=[xt.b[xb]], writes=[dst_bufs[i]])
        S.dma("sp", DMA(cfin_ap, Cst[:]), reads=[Cst.b])
        S.barrier()
        S.emit()


def mlstm_consts():
    s_ = np.arange(128)
    tri = (s_[:, None] <= s_[None, :]).astype(np.float32)
    return {"tri": tri, "ones128": np.ones((128, 128), np.float32)}


def kv_stage(K, src_tiles, src_bufs, kvd, kv_norm_ap, kv_w_ap, cmp_pe_ap, cmp_w1_ap, cmp_w2_ap, k_norm_ap):
    nc, S = K.nc, K.S
    NT = len(src_tiles)
    NTOK = NT * 128
    NC = (NTOK - 32) // 16 + 1
    NCH = max(1, NT // 16)
    with contextlib.ExitStack() as ea:
        wkv = K.sb(ea, "wkv", [128, 8, 1536], BF16)
        gain = K.sb(ea, "gain", [128, 1024], F32)
        kng = K.sb(ea, "kng", [128, 3, 64], F32)
        w1 = K.sb(ea, "w1", [64, 2, 32, 256], BF16)
        w2 = K.sb(ea, "w2", [128, 2, 2, 64], BF16)
        peT = K.sb(ea, "peT", [64, 2, 32], BF16)
        pef = K.sb(ea, "pef", [32, 2, 64], F32)
        peb = K.sb(ea, "peb", [32, 2, 64], BF16)
        aT = K.sb(ea, "aT", [64, 2, 4, NTOK], BF16, nbuf=NT)
        xt = K.sb(ea, "xt", [128, 2, 1024], F32, nbuf=2)
        hnT = K.sb(ea, "hnT", [128, 2, 8, 128], BF16, nbuf=2)
        wk = {"junk": K.sb(ea, "junk", [128, 1024], F32), "ss": K.sb(ea, "ss", [128, 1], F32),
              "tmp": K.sb(ea, "tmp", [128, 1], F32), "rstd": K.sb(ea, "rstd", [128, 1], F32),
              "hn": K.sb(ea, "hn", [128, 2, 1024], BF16, nbuf=2), "ctr": 0}
        sqk = K.sb(ea, "sqk", [128, 2, 4, 64], F32)
        ssk = K.sb(ea, "ssk", [128, 8], F32)
        rsk = K.sb(ea, "rsk", [128, 8], F32)
        knf = K.sb(ea, "knf", [128, 2, 4, 64], F32)
        knb = K.sb(ea, "knb", [128, 2, 2, 256], BF16, nbuf=2)
        vb = K.sb(ea, "vb", [128, 2, 2, 256], BF16, nbuf=2)
        cb = K.sb(ea, "cb", [128, 2, 512], BF16, nbuf=2)
        kTs = K.sb(ea, "kTs", [128, 2, 4, 128], BF16, nbuf=2)
        bias = K.sb(ea, "bias", [128, 2, 2], F32)
        hT = K.sb(ea, "hT", [128, 2, NCH * 128], BF16)
        oc = K.sb(ea, "oc", [128, 64], F32)
        ocb = K.sb(ea, "ocb", [128, 128], BF16)
        kcs = K.sb(ea, "kcs", [64, 128], BF16)
        sqc = K.sb(ea, "sqc", [128, 64], F32)
        ssc = K.sb(ea, "ssc", [128, 1], F32)
        rsc = K.sb(ea, "rsc", [128, 1], F32)
        pT = K.ps(ea, "pT", [128, 8, 128], BF16)
        pT2 = K.ps(ea, "pT2", [128, 8, 128], BF16)
        pp = K.ps(ea, "pp", [128, 4, 512], F32, nbuf=4)
        pb_ = K.ps(ea, "pb_", [128, 512], F32)
        bank = [0]

        def nb():
            b = bank[0] % 4
            bank[0] += 1
            return b

        S.dma("pool", DMA(wkv[:], kv_w_ap.rearrange("(k p) f -> p k f", p=128)), writes=[wkv.b])
        S.dma("sp", DMA(gain[:], kv_norm_ap.partition_broadcast(128)), writes=[gain.b])
        S.dma("sp", DMA(kng[:].rearrange("p a d -> p (a d)"), k_norm_ap.rearrange("a d -> (a d)").partition_broadcast(128)),
              writes=[kng.b])
        for ty in range(2):
            S.dma("pool", DMA(w1[:, ty, :, :], cmp_w1_ap[ty].rearrange("(l d) f -> d l f", d=64)), writes=[w1.b])
            S.dma("pool", DMA(w2[:, ty, :, :], cmp_w2_ap[ty].rearrange("(a p) o -> p a o", p=128)), writes=[w2.b])
            S.dma("sp", DMA(pef[:, ty, :], cmp_pe_ap[ty]), writes=[pef.b])
        S.op("dve", CP(peb[:], pef[:]), reads=[pef.b], writes=[peb.b])
        S.op("dve", MSET(hT[:], 0.0), writes=[hT.b])
        S.op("dve", MSET(ocb[:], 0.0), writes=[ocb.b])
        for ty in range(2):
            S.op("pe", TR(pT2[0:64, ty, 0:32], peb[:, ty, :], K.ident[0:32, 0:32]), reads=[peb.b, K.ident.b], writes=[pT2.b])
        S.op("act", ACP(peT[:], pT2[0:64, 0:2, 0:32]), reads=[pT2.b], writes=[peT.b])
        for i in range(NT):
            xb = i % 2
            S.dma("sp", DMA(xt[:, xb, :], src_tiles[i]), reads=[src_bufs[i]], writes=[xt.b[xb]])
            norm_transpose_tile(K, xt[:, xb, :], [xt.b[xb]], gain, wk, hnT[:, xb, :, :], hnT.b[xb], pT)
            hb = hnT.b[xb]
            banks = []
            for cg in range(3):
                b = nb()
                banks.append(b)
                for k in range(8):
                    S.op("pe", MM(pp[:, b, :], hnT[:, xb, k, :], wkv[:, k, cg * 512:(cg + 1) * 512], k == 0, k == 7),
                         reads=[wkv.b, hb], writes=[pp.b[b]])
            bc_, bs_, bw_ = banks
            S.op("act", ACP(cb[:, xb, :], pp[:, bc_, :]), reads=[pp.b[bc_]], writes=[cb.b[xb]])
            for j in range(8):
                S.op("pe", TR(pT2[0:64, j, :], cb[:, xb, j * 64:(j + 1) * 64], K.ident[:]),
                     reads=[cb.b[xb], K.ident.b], writes=[pT2.b])
            S.op("act", ACP(aT[:, :, :, i * 128:(i + 1) * 128], pT2[0:64, :, :].rearrange("p (a h) t -> p a h t", a=2)),
                 reads=[pT2.b], writes=[aT.b[i]])
            for a, bsrc in enumerate((bs_, bw_)):
                S.op("act", ACTF(sqk[:, a, :, :], pp[:, bsrc, 0:256].rearrange("p (h d) -> p h d", h=4), AF.Square),
                     reads=[pp.b[bsrc]], writes=[sqk.b])
            S.op("dve", lambda e: e.tensor_reduce(out=ssk[:], in_=sqk[:].rearrange("p a h d -> p (a h) d"), axis=AX.X, op=ALU.add),
                 reads=[sqk.b], writes=[ssk.b])
            S.op("act", ACTF(rsk[:], ssk[:], AF.Sqrt, bias=K.epsc[:, 0:1], scale=1.0 / 64), reads=[ssk.b, K.epsc.b], writes=[rsk.b])
            S.op("dve", lambda e: e.reciprocal(out=ssk[:], in_=rsk[:]), reads=[rsk.b], writes=[ssk.b])
            for a, bsrc in enumerate((bs_, bw_)):
                S.op("dve", TT(knf[:, a, :, :], pp[:, bsrc, 0:256].rearrange("p (h d) -> p h d", h=4),
                               bc_last(ssk[:, a * 4:(a + 1) * 4], 64), ALU.mult), reads=[pp.b[bsrc], ssk.b], writes=[knf.b])
                S.op("dve", TT(knb[:, xb, a, :].rearrange("p (h d) -> p h d", h=4), knf[:, a, :, :],
                               bc_mid(kng[:, 1 + a, :], 4), ALU.mult), reads=[knf.b, kng.b], writes=[knb.b[xb]])
                S.op("act", ACP(vb[:, xb, a, :], pp[:, bsrc, 256:512]), reads=[pp.b[bsrc]], writes=[vb.b[xb]])
            for a in range(2):
                for j in range(2):
                    S.op("pe", TR(pT[:, a * 2 + j, :], knb[:, xb, a, j * 128:(j + 1) * 128], K.ident[:]),
                         reads=[knb.b[xb], K.ident.b], writes=[pT.b])
            S.op("act", ACP(kTs[:, xb, :, :], pT[:, 0:4, :]), reads=[pT.b], writes=[kTs.b[xb]])
            for a, nm in enumerate(("ksT", "kwT")):
                for j in range(2):
                    S.dma("sp", DMA(kvd[nm][j * 128:(j + 1) * 128, i * 128:(i + 1) * 128], kTs[:, xb, a * 2 + j, :]),
                          reads=[kTs.b[xb]], writes=[kvd["buf"]])
            for a, nm in enumerate(("vs", "vw")):
                S.dma("sp", DMA(kvd[nm][i * 128:(i + 1) * 128, :], vb[:, xb, a, :]), reads=[vb.b[xb]], writes=[kvd["buf"]])
        for ty in range(2):
            for fh in range(2):
                for l in range(32):
                    S.op("pe", MM(pb_[:, fh:fh + 1], w1[:, ty, l, fh * 128:(fh + 1) * 128], peT[:, ty, l:l + 1], l == 0, l == 31),
                         reads=[w1.b, peT.b], writes=[pb_.b])
            S.op("dve", CP(bias[:, ty, :], pb_[:, 0:2]), reads=[pb_.b], writes=[bias.b])
            for h in range(4):
                for fh in range(2):
                    b = nb()
                    for l in range(32):
                        rhs = aT[:, ty, h, l:l + 16 * (NC - 1) + 1:16]
                        S.op("pe", MM(pp[:, b, 0:NC], w1[:, ty, l, fh * 128:(fh + 1) * 128], rhs, l == 0, l == 31),
                             reads=[w1.b] + list(aT.b), writes=[pp.b[b]])
                    S.op("act", ACTF(hT[:, fh, 0:NC], pp[:, b, 0:NC], AF.Silu, bias=bias[:, ty, fh:fh + 1]),
                         reads=[pp.b[b], bias.b], writes=[hT.b])
                for ch in range(NCH):
                    b = nb()
                    for fh in range(2):
                        S.op("pe", MM(pp[:, b, 0:64], hT[:, fh, ch * 128:(ch + 1) * 128], w2[:, ty, fh, :], fh == 0, fh == 1),
                             reads=[hT.b, w2.b], writes=[pp.b[b]])
                    if ty == 0:
                        S.op("dve", MSET(ssc[:], 0.0), writes=[ssc.b])
                        S.op("act", ACTF(sqc[:], pp[:, b, 0:64], AF.Square, accum_out=ssc[:, 0:1]),
                             reads=[pp.b[b], ssc.b], writes=[ssc.b, sqc.b])
                        S.op("act", ACTF(rsc[:], ssc[:], AF.Sqrt, bias=K.epsc[:, 0:1], scale=1.0 / 64),
                             reads=[ssc.b, K.epsc.b], writes=[rsc.b])
                        S.op("dve", lambda e: e.reciprocal(out=ssc[:], in_=rsc[:]), reads=[rsc.b], writes=[ssc.b])
                        S.op("dve", STT(ocb[:, 0:64], pp[:, b, 0:64], ssc[:, 0:1], kng[:, 0, :], ALU.mult, ALU.mult),
                             reads=[pp.b[b], ssc.b, kng.b], writes=[ocb.b])
                        S.op("pe", TR(pT2[:, 0, :], ocb[:], K.ident[:]), reads=[ocb.b, K.ident.b], writes=[pT2.b])
                        S.op("act", ACP(kcs[:], pT2[0:64, 0, :]), reads=[pT2.b], writes=[kcs.b])
                        S.dma("sp", DMA(kvd["kcT"][h, :, ch * 128:(ch + 1) * 128], kcs[:]), reads=[kcs.b], writes=[kvd["buf"]])
                    else:
                        S.op("act", ACP(ocb[:, 64:128], pp[:, b, 0:64]), reads=[pp.b[b]], writes=[ocb.b])
                        S.dma("sp", DMA(kvd["vcm"][ch * 128:(ch + 1) * 128, h, :], ocb[:, 64:128]), reads=[ocb.b],
                              writes=[kvd["buf"]])
        S.barrier()
        S.emit()


def nsa_consts(NTF, p):
    NTOK = NTF * 128
    NI = NTF // 2
    n_cmp = (NTOK - 32) // 16 + 1
    NCH = max(1, NTF // 16)
    n_sel = NTOK // 64
    j = np.arange(n_sel)
    emat = (np.arange(NTOK)[None, :] // 64 == j[:, None]).astype(np.float32)
    if n_sel < 64:
        emat = np.concatenate([emat, np.zeros((64 - n_sel, NTOK), np.float32)], 0)
    c0 = np.arange(n_cmp) * 16
    s0 = j * 64
    ov = ((c0[:, None] < s0[None, :] + 64) & (c0[:, None] + 32 > s0[None, :])).astype(np.float32)
    ovl = np.zeros((NCH * 128, 64), np.float32)
    ovl[:n_cmp, :n_sel] = ov
    tl = np.arange(128)
    cmask = np.zeros((NI, NCH, 128, 128), np.float32)
    vam = np.zeros((NI, 2, 128, 64), np.float32)
    for i in range(NI):
        gi = 2 * i + p
        t = gi * 128 + tl
        c = np.arange(NCH * 128)
        ok = (16 * c[:, None] + 31 <= t[None, :]) & (c[:, None] < n_cmp)
        cmask[i] = np.where(ok, 0.0, NEGM).reshape(NCH, 128, 128)
        jj = np.arange(64)
        valid = (jj[None, :] * 64 <= t[:, None]) & (jj[None, :] < n_sel)
        cur = (t // 64)[:, None]
        f0 = (jj[None, :] == 0)
        f1 = (jj[None, :] == cur) & ~f0
        f2 = (jj[None, :] == cur - 1) & ~f0
        forced = (f0 | f1 | f2) & valid
        vm = (valid & ~forced).astype(np.float32)
        am = np.where(forced, 1e4 + 1.0 * f1 + 2.0 * f2, np.where(valid, 0.0, -1.0 - jj[None, :] / 64.0))
        vam[i, 0] = vm
        vam[i, 1] = am
    sl = np.arange(128)[:, None]
    causal = np.where(sl <= tl[None, :], 0.0, NEGM).astype(np.float32)
    lower = np.where(sl > tl[None, :], 0.0, NEGM).astype(np.float32)
    allneg = np.full((128, 128), NEGM, np.float32)
    zero = np.zeros((128, 128), np.float32)
    if p == 0:
        dmask = np.stack([causal, allneg])
        wmask = np.stack([lower, zero, zero, zero, causal, allneg])
    else:
        dmask = np.stack([zero, causal])
        wmask = np.stack([allneg, lower, zero, zero, zero, causal])
    return {"emat": emat, "ovl": ovl, "cmask": cmask, "vam": vam.astype(np.float32), "dmask": dmask, "wmask": wmask}


def nsa_stage(K, NTF, src_tiles, src_bufs, dst_tiles, dst_bufs, kvd, gain_ap, w_in_ap, q_norm_ap, w_out_ap, cst):
    nc, S = K.nc, K.S
    NI = len(src_tiles)
    NTOK = NTF * 128
    NCH = max(1, NTF // 16)
    TINY = 1e-30
    with contextlib.ExitStack() as ea:
        Kaug = K.sb(ea, "Kaug", [128, 4, NTOK], BF16)
        kwA = K.sb(ea, "kwA", [128, 4, NTOK], BF16)
        kcA = K.sb(ea, "kcA", [128, 4, NCH * 128], BF16)
        vsx = K.sb(ea, "vsx", [128, NTF, 4, 66], BF16)
        vwx = K.sb(ea, "vwx", [128, NTF, 4, 66], BF16)
        vcx = K.sb(ea, "vcx", [128, NCH, 4, 130], BF16)
        win = K.sb(ea, "win", [128, 8, 1072], BF16)
        wout = K.sb(ea, "wout", [128, 8, 1024], BF16)
        gain = K.sb(ea, "gain", [128, 1024], F32)
        qg = K.sb(ea, "qg", [128, 64], F32)
        onec = K.sb(ea, "onec", [128, 512], F32)
        dmask4 = K.sb(ea, "dmask4", [128, 2, 4, 128], BF16)
        wmask4 = K.sb(ea, "wmask4", [128, 6, 4, 128], BF16)
        cmask4 = K.sb(ea, "cmask4", [128, 2, NCH, 4, 128], BF16, nbuf=2)
        vam = K.sb(ea, "vam", [128, 2, 2, 64], F32, nbuf=2)
        xt = K.sb(ea, "xt", [128, 2, 1024], F32, nbuf=2)
        hnT = K.sb(ea, "hnT", [128, 8, 128], BF16)
        wk = {"junk": K.sb(ea, "junk", [128, 1024], F32), "ss": K.sb(ea, "ss", [128, 1], F32),
              "tmp": K.sb(ea, "tmp", [128, 1], F32), "rstd": K.sb(ea, "rstd", [128, 1], F32),
              "hn": K.sb(ea, "hn", [128, 2, 1024], BF16, nbuf=2), "ctr": 0}
        qf = K.sb(ea, "qf", [128, 16, 64], F32)
        sqq = K.sb(ea, "sqq", [128, 16, 64], F32)
        ssq = K.sb(ea, "ssq", [128, 16], F32)
        rsq = K.sb(ea, "rsq", [128, 16], F32)
        qb = K.sb(ea, "qb", [128, 16, 64], BF16)
        gts = K.sb(ea, "gts", [128, 48], F32)
        Qaug = K.sb(ea, "Qaug", [128, 4, 4, 128], BF16, nbuf=4)
        Ec = K.sb(ea, "Ec", [128, NCH, 512], BF16)
        Pt = K.sb(ea, "Pt", [128, 3, 512], BF16, nbuf=3)
        oacc = K.sb(ea, "oacc", [128, 16, 64], F32)
        ob = K.sb(ea, "ob", [128, 1024], BF16)
        oT = K.sb(ea, "oT", [128, 8, 128], BF16)
        z4 = K.sb(ea, "z4", [128, 4], F32)
        rz4 = K.sb(ea, "rz4", [128, 4], F32)
        gz4 = K.sb(ea, "gz4", [128, 4], F32)
        tmp4 = K.sb(ea, "tmp4", [128, 4, 64], F32)
        imp = K.sb(ea, "imp", [128, 64], F32)
        score = K.sb(ea, "score", [128, 64], F32)
        sc2 = K.sb(ea, "sc2", [128, 64], F32)
        m8 = K.sb(ea, "m8", [128, 16], F32)
        selm = K.sb(ea, "selm", [128, 64], F32)
        valm = K.sb(ea, "valm", [128, 64], F32)
        nsp = K.sb(ea, "nsp", [128, 128], BF16)
        pT = K.ps(ea, "pT", [128, 8, 128], BF16)
        pS = K.ps(ea, "pS", [128, 3, 512], F32, nbuf=3)
        pOc = K.ps(ea, "pOc", [128, 4, 256], F32)
        pOs = K.ps(ea, "pOs", [128, 4, 128], F32)
        pOw = K.ps(ea, "pOw", [128, 4, 128], F32)
        bank = [0]
        pcnt = [0]

        def nbS():
            b = bank[0] % 3
            bank[0] += 1
            return b

        kb = kvd["buf"]
        S.op("dve", MSET(onec[:], 1.0), writes=[onec.b])
        S.op("pool", MSET(kwA[64:128, :, :], 0.0), writes=[kwA.b])
        S.op("pool", MSET(kcA[64:128, :, :], 0.0), writes=[kcA.b])
        for h in range(4):
            S.dma("sp", DMA(Kaug[0:64, h, :], kvd["ksT"][h * 64:(h + 1) * 64, :]), reads=[kb], writes=[Kaug.b])
            S.dma("pool", DMA(Kaug[64:128, h, :], cst["emat"]), writes=[Kaug.b])
            S.dma("sp", DMA(kwA[0:64, h, :], kvd["kwT"][h * 64:(h + 1) * 64, :]), reads=[kb], writes=[kwA.b])
            S.dma("sp", DMA(kcA[0:64, h, :], kvd["kcT"][h]), reads=[kb], writes=[kcA.b])
        for h in range(4):
            S.dma("sp", DMA(vsx[:, :, h, 0:64], kvd["vs"].rearrange("(kt p) (h d) -> p kt h d", p=128, h=4)[:, :, h, :]),
                  reads=[kb], writes=[vsx.b])
            S.dma("sp", DMA(vwx[:, :, h, 0:64], kvd["vw"].rearrange("(kt p) (h d) -> p kt h d", p=128, h=4)[:, :, h, :]),
                  reads=[kb], writes=[vwx.b])
            S.dma("sp", DMA(vcx[:, :, h, 0:64], kvd["vcm"].rearrange("(c p) h d -> p c h d", p=128)[:, :, h, :]),
                  reads=[kb], writes=[vcx.b])
        for h in range(4):
            S.dma("pool", DMA(vcx[:, :, h, 65:129], cst["ovl"].rearrange("(c p) j -> p c j", p=128)), writes=[vcx.b])
        S.op("dve", CP(vsx[:, :, :, 64:65], onec[:, 0:NTF * 4].rearrange("p (a h o) -> p a h o", h=4, o=1)),
             reads=[onec.b], writes=[vsx.b])
        S.op("dve", CP(vwx[:, :, :, 64:65], onec[:, 0:NTF * 4].rearrange("p (a h o) -> p a h o", h=4, o=1)),
             reads=[onec.b], writes=[vwx.b])
        S.op("dve", CP(vcx[:, :, :, 64:65], onec[:, 0:NCH * 4].rearrange("p (a h o) -> p a h o", h=4, o=1)),
             reads=[onec.b], writes=[vcx.b])
        S.dma("pool", DMA(win[:], w_in_ap.rearrange("(k p) f -> p k f", p=128)), writes=[win.b])
        S.dma("pool", DMA(wout[:], w_out_ap.rearrange("(k p) n -> p k n", p=128)), writes=[wout.b])
        S.dma("sp", DMA(gain[:], gain_ap.partition_broadcast(128)), writes=[gain.b])
        S.dma("sp", DMA(qg[:], q_norm_ap.partition_broadcast(128)), writes=[qg.b])
        S.op("dve", TS(qg[:], qg[:], 0.125, None, ALU.mult), reads=[qg.b], writes=[qg.b])
        for sl_ in range(2):
            S.dma("pool", DMA(dmask4[:, sl_, :, :], cst["dmask"][sl_].unsqueeze(1).to_broadcast([128, 4, 128])), writes=[dmask4.b])
        for sl_ in range(6):
            S.dma("pool", DMA(wmask4[:, sl_, :, :], cst["wmask"][sl_].unsqueeze(1).to_broadcast([128, 4, 128])), writes=[wmask4.b])
        S.op("pool", MSET(Qaug[:].rearrange("p h g t -> p (h g t)"), 0.0), writes=list(Qaug.b))
        S.op("pool", MSET(nsp[:], 0.0), writes=[nsp.b])

        def attend(h, Kt, Vx, pO, kts, mask_of, xb_):
            for n_, kt in enumerate(kts):
                b = nbS()
                mk = mask_of(kt)
                S.op("pe", MM(pS[:, b, :], Kt[:, h, kt * 128:(kt + 1) * 128], Qaug[:, h, :, :].rearrange("p g t -> p (g t)"),
                              True, mk is None), reads=[Kt.b, Qaug.b[h]], writes=[pS.b[b]])
                if mk is not None:
                    S.op("pe", MM(pS[:, b, :], K.ident[:], mk[0], False, True), reads=[K.ident.b, mk[1]], writes=[pS.b[b]])
                pb = pcnt[0] % 3
                pcnt[0] += 1
                S.op("act", ACTF(Pt[:, pb, :], pS[:, b, :], AF.Exp), reads=[pS.b[b]], writes=[Pt.b[pb]])
                for g in range(4):
                    S.op("pe", lambda e, o=pO[:, g, 0:65], l=Pt[:, pb, g * 128:(g + 1) * 128], r=Vx[:, kt, h, 0:65],
                         st=(n_ == 0 and g == 0), sp=(n_ == len(kts) - 1):
                         e.matmul(o, lhsT=l, rhs=r, start=st, stop=sp, skip_group_check=True),
                         reads=[Pt.b[pb], Vx.b], writes=[pO.b])

        def finish(pO, h, br, first):
            S.op("dve", TS(z4[:], pO[:, :, 64], TINY, None, ALU.max), reads=[pO.b], writes=[z4.b])
            S.op("dve", lambda e: e.reciprocal(out=rz4[:], in_=z4[:]), reads=[z4.b], writes=[rz4.b])
            S.op("dve", TT(gz4[:], rz4[:], gts[:, br * 16 + h * 4: br * 16 + h * 4 + 4], ALU.mult),
                 reads=[rz4.b, gts.b], writes=[gz4.b])
            if first:
                S.op("dve", TT(oacc[:, h * 4:(h + 1) * 4, :], pO[:, :, 0:64], bc_last(gz4[:], 64), ALU.mult),
                     reads=[pO.b, gz4.b], writes=[oacc.b])
            else:
                S.op("dve", TT(tmp4[:], pO[:, :, 0:64], bc_last(gz4[:], 64), ALU.mult),
                     reads=[pO.b, gz4.b], writes=[tmp4.b])
                S.op("dve", TT(oacc[:, h * 4:(h + 1) * 4, :], oacc[:, h * 4:(h + 1) * 4, :], tmp4[:], ALU.add),
                     reads=[tmp4.b, oacc.b], writes=[oacc.b])

        for i in range(NI):
            xb = i % 2
            S.dma("sp", DMA(xt[:, xb, :], src_tiles[i]), reads=[src_bufs[i]], writes=[xt.b[xb]])
            for ch in range(NCH):
                S.dma("pool", DMA(cmask4[:, xb, ch, :, :], cst["cmask"][i, ch].unsqueeze(1).to_broadcast([128, 4, 128])),
                      writes=[cmask4.b[xb]])
            S.dma("sp", DMA(vam[:, xb, :, :], cst["vam"][i].rearrange("a p j -> p a j")), writes=[vam.b[xb]])
            norm_transpose_tile(K, xt[:, xb, :], [xt.b[xb]], gain, wk, hnT[:], hnT.b, pT)
            for cg, (c0, n) in enumerate(((0, 512), (512, 512), (1024, 48))):
                b = nbS()
                for k in range(8):
                    S.op("pe", MM(pS[:, b, 0:n], hnT[:, k, :], win[:, k, c0:c0 + n], k == 0, k == 7),
                         reads=[hnT.b, win.b], writes=[pS.b[b]])
                if cg < 2:
                    S.op("act", ACP(qf[:, cg * 8:(cg + 1) * 8, :], pS[:, b, :].rearrange("p (h d) -> p h d", h=8)),
                         reads=[pS.b[b]], writes=[qf.b])
                    S.op("act", ACTF(sqq[:, cg * 8:(cg + 1) * 8, :], pS[:, b, :].rearrange("p (h d) -> p h d", h=8), AF.Square),
                         reads=[pS.b[b]], writes=[sqq.b])
                else:
                    S.op("act", ACTF(gts[:], pS[:, b, 0:48], AF.Sigmoid), reads=[pS.b[b]], writes=[gts.b])
            S.op("dve", lambda e: e.tensor_reduce(out=ssq[:], in_=sqq[:], axis=AX.X, op=ALU.add), reads=[sqq.b], writes=[ssq.b])
            S.op("act", ACTF(rsq[:], ssq[:], AF.Sqrt, bias=K.epsc[:, 0:1], scale=1.0 / 64), reads=[ssq.b, K.epsc.b], writes=[rsq.b])
            S.op("dve", lambda e: e.reciprocal(out=ssq[:], in_=rsq[:]), reads=[rsq.b], writes=[ssq.b])
            S.op("dve", TT(sqq[:], qf[:], bc_last(ssq[:], 64), ALU.mult), reads=[qf.b, ssq.b], writes=[sqq.b])
            S.op("dve", TT(qb[:], sqq[:], bc_mid(qg[:], 16), ALU.mult), reads=[sqq.b, qg.b], writes=[qb.b])
            for half in range(2):
                for hq in range(8):
                    S.op("pe", TR(pT[0:64, hq, :], qb[:, half * 8 + hq, :], K.ident[:]), reads=[qb.b, K.ident.b], writes=[pT.b])
                S.op("act", ACP(Qaug[0:64, 2 * half:2 * half + 2, :, :].rearrange("p h g t -> p (h g) t"), pT[0:64, :, :]),
                     reads=[pT.b], writes=[Qaug.b[2 * half], Qaug.b[2 * half + 1]])
            for h in range(4):
                for ch in range(NCH):
                    b = nbS()
                    S.op("pe", MM(pS[:, b, :], kcA[:, h, ch * 128:(ch + 1) * 128], Qaug[:, h, :, :].rearrange("p g t -> p (g t)"),
                                  True, False), reads=[kcA.b, Qaug.b[h]], writes=[pS.b[b]])
                    S.op("pe", MM(pS[:, b, :], K.ident[:], cmask4[:, xb, ch, :, :].rearrange("p g t -> p (g t)"), False, True),
                         reads=[K.ident.b, cmask4.b[xb]], writes=[pS.b[b]])
                    S.op("act", ACTF(Ec[:, ch, :], pS[:, b, :], AF.Exp), reads=[pS.b[b]], writes=[Ec.b])
                for g in range(4):
                    for ch in range(NCH):
                        S.op("pe", MM(pOc[:, g, 0:129], Ec[:, ch, g * 128:(g + 1) * 128], vcx[:, ch, h, 0:129],
                                      ch == 0, ch == NCH - 1), reads=[Ec.b, vcx.b], writes=[pOc.b])
                finish(pOc, h, 0, True)
                S.op("dve", TT(tmp4[:], pOc[:, :, 65:129], bc_last(rz4[:], 64), ALU.mult), reads=[pOc.b, rz4.b], writes=[tmp4.b])
                S.op("dve", lambda e: e.tensor_reduce(out=imp[:], in_=tmp4[:].rearrange("p g j -> p j g"), axis=AX.X, op=ALU.add),
                     reads=[tmp4.b], writes=[imp.b])
                S.op("dve", TT(score[:], imp[:], vam[:, xb, 0, :], ALU.mult), reads=[imp.b, vam.b[xb]], writes=[score.b])
                S.op("dve", TT(score[:], score[:], vam[:, xb, 1, :], ALU.add), reads=[score.b, vam.b[xb]], writes=[score.b])
                S.op("dve", lambda e: e.max(out=m8[:, 0:8], in_=score[:]), reads=[score.b], writes=[m8.b])
                S.op("dve", lambda e: e.match_replace(out=sc2[:], in_to_replace=m8[:, 0:8], in_values=score[:], imm_value=-3.0),
                     reads=[score.b, m8.b], writes=[sc2.b])
                S.op("dve", lambda e: e.max(out=m8[:, 8:16], in_=sc2[:]), reads=[sc2.b], writes=[m8.b])
                S.op("dve", TS(selm[:], score[:], m8[:, 15:16], None, ALU.is_ge), reads=[score.b, m8.b], writes=[selm.b])
                S.op("dve", TS(valm[:], score[:], 0.0, None, ALU.is_ge), reads=[score.b], writes=[valm.b])
                S.op("dve", TT(selm[:], selm[:], valm[:], ALU.mult), reads=[selm.b, valm.b], writes=[selm.b])
                S.op("dve", TS(nsp[:, 64:128], selm[:], -NEGM, NEGM, ALU.mult, ALU.add), reads=[selm.b], writes=[nsp.b])
                S.op("pe", TR(pT[:, 0, :], nsp[:], K.ident[:]), reads=[nsp.b, K.ident.b], writes=[pT.b])
                S.op("act", ACP(Qaug[64:128, h, :, :], pT[64:128, 0, :].unsqueeze(1).to_broadcast([64, 4, 128])),
                     reads=[pT.b], writes=[Qaug.b[h]])
                kts = list(range(2 * i + 2))
                attend(h, Kaug, vsx, pOs, kts,
                       lambda kt: ((dmask4[:, kt - 2 * i, :, :].rearrange("p g t -> p (g t)"), dmask4.b) if kt >= 2 * i else None), xb)
                finish(pOs, h, 1, False)
                slots = [s_ for s_ in range(6) if 2 * i - 4 + s_ >= 0]
                kts = [2 * i - 4 + s_ for s_ in slots]
                attend(h, kwA, vwx, pOw, kts,
                       lambda kt: ((wmask4[:, kt - (2 * i - 4), :, :].rearrange("p g t -> p (g t)"), wmask4.b)
                                   if (kt - (2 * i - 4)) in (0, 1, 4, 5) else None), xb)
                finish(pOw, h, 2, False)
            S.op("dve", CP(ob[:], oacc[:].rearrange("p h d -> p (h d)")), reads=[oacc.b], writes=[ob.b])
            for k in range(8):
                S.op("pe", TR(pT[:, k, :], ob[:, k * 128:(k + 1) * 128], K.ident[:]), reads=[ob.b, K.ident.b], writes=[pT.b])
            S.op("act", ACP(oT[:], pT[:]), reads=[pT.b], writes=[oT.b])
            for half in range(2):
                b = nbS()
                for k in range(8):
                    S.op("pe", MM(pS[:, b, :], oT[:, k, :], wout[:, k, half * 512:(half + 1) * 512], k == 0, k == 7),
                         reads=[oT.b, wout.b], writes=[pS.b[b]])
                S.op("dve", TT(xt[:, xb, half * 512:(half + 1) * 512], xt[:, xb, half * 512:(half + 1) * 512], pS[:, b, :], ALU.add),
                     reads=[xt.b[xb], pS.b[b]], writes=[xt.b[xb]])
            S.dma("sp", DMA(dst_tiles[i], xt[:, xb, :]), reads=[xt.b[xb]], writes=[dst_bufs[i]])
        S.barrier()
        S.emit()


def _din(nc, name, shape, dt=F32):
    return nc.dram_tensor(name, list(shape), dt, kind="ExternalInput").ap()


def _dout(nc, name, shape, dt=F32):
    return nc.dram_tensor(name, list(shape), dt, kind="ExternalOutput").ap()


def _dint(nc, name, shape, dt=F32):
    return nc.dram_tensor(name, list(shape), dt, kind="Internal").ap()


def tiles_of(ap, n):
    return [ap[i * 128:(i + 1) * 128, :] for i in range(n)]


def build_l0(debug=False, stages="fmf", mphase="AB"):
    nc = bass.Bass("TRN2", target_bir_lowering=False)
    x = _din(nc, "x", [2048, 1024])
    ident = _din(nc, "ident", [128, 128])
    ffn_norm = _din(nc, "ffn_norm", [2, 1024])
    ffn_w_in = _din(nc, "ffn_w_in", [2, 1024, 5632])
    ffn_w_out = _din(nc, "ffn_w_out", [2, 2816, 1024])
    mix_norm = _din(nc, "mix_norm", [1024])
    a_w_in = _din(nc, "a_w_in", [1024, 3088])
    a_b_if = _din(nc, "a_b_if", [2, 8])
    a_g_head = _din(nc, "a_g_head", [8, 128])
    a_w_out = _din(nc, "a_w_out", [1024, 1024])
    cinit = _din(nc, "cinit", [128, 4, 129])
    cst = {k: _din(nc, k, v.shape) for k, v in mlstm_consts().items()}
    h3 = _dout(nc, "h3", [2048, 1024])
    cfin = _dout(nc, "cfin", [128, 4, 129])
    if debug:
        h1 = _dout(nc, "h1", [2048, 1024])
        h2 = _dout(nc, "h2", [2048, 1024])
    else:
        h1 = _dint(nc, "h1", [2048, 1024])
        h2 = _dint(nc, "h2", [2048, 1024])
    K = KB(nc)
    with contextlib.ExitStack() as es:
        load_consts(K, es, ident)
        bx = [Buf() for _ in range(16)]
        b1 = [Buf() for _ in range(16)]
        b2 = [Buf() for _ in range(16)]
        b3 = [Buf() for _ in range(16)]
        if stages[0] == "f":
            ffn_stage(K, tiles_of(x, 16), bx, tiles_of(h1, 16), b1, ffn_norm[0], ffn_w_in[0], ffn_w_out[0])
        if stages[1] == "m":
            mlstm_stage(K, tiles_of(h1 if stages[0] == "f" else x, 16), b1, tiles_of(h2, 16), b2, mix_norm, a_w_in, a_b_if, a_g_head, a_w_out,
                        cinit, cfin, cst, mphase=mphase)
        if stages[2] == "f":
            ffn_stage(K, tiles_of(h2, 16), b2, tiles_of(h3, 16), b3, ffn_norm[1], ffn_w_in[1], ffn_w_out[1])
    K.S.close()
    return nc


def l0_inputs(inp, xs, cinit):
    d = {"x": xs, "ident": np.eye(128, dtype=np.float32), "ffn_norm": inp["ffn_norm"][0],
         "ffn_w_in": inp["ffn_w_in"][0], "ffn_w_out": inp["ffn_w_out"][0], "mix_norm": inp["mix_norm"][0],
         "a_w_in": inp["a_w_in"][0], "a_b_if": inp["a_b_if"][0], "a_g_head": inp["a_g_head"][0],
         "a_w_out": inp["a_w_out"][0], "cinit": cinit}
    d.update(mlstm_consts())
    return d


def build_l1(NTF=32, debug=False, stages="kfnf"):
    NI = NTF // 2
    NTOK = NTF * 128
    NCH = max(1, NTF // 16)
    nc = bass.Bass("TRN2", target_bir_lowering=False)
    h3full = _din(nc, "h3full", [NTOK, 1024])
    h3own = _din(nc, "h3own", [NI * 128, 1024])
    ident = _din(nc, "ident", [128, 128])
    ffn_norm = _din(nc, "ffn_norm", [2, 1024])
    ffn_w_in = _din(nc, "ffn_w_in", [2, 1024, 5632])
    ffn_w_out = _din(nc, "ffn_w_out", [2, 2816, 1024])
    mix_norm = _din(nc, "mix_norm", [1024])
    kv_norm = _din(nc, "kv_norm", [1024])
    kv_w = _din(nc, "kv_w", [1024, 1536])
    cmp_pe = _din(nc, "cmp_pe", [2, 32, 64])
    cmp_w1 = _din(nc, "cmp_w1", [2, 2048, 256])
    cmp_w2 = _din(nc, "cmp_w2", [2, 256, 64])
    k_norm = _din(nc, "k_norm", [3, 64])
    b_w_in = _din(nc, "b_w_in", [1024, 1072])
    b_q_norm = _din(nc, "b_q_norm", [64])
    b_w_out = _din(nc, "b_w_out", [1024, 1024])
    cst = {k: _din(nc, k, v.shape) for k, v in nsa_consts(NTF, 0).items()}
    out = _dout(nc, "out", [NI * 128, 1024])
    mk = _dout if debug else _dint
    h4 = mk(nc, "h4", [NI * 128, 1024])
    h5 = mk(nc, "h5", [NI * 128, 1024])
    kvd = {"ksT": mk(nc, "ksT", [256, NTOK], BF16), "kwT": mk(nc, "kwT", [256, NTOK], BF16),
           "vs": mk(nc, "vs", [NTOK, 256], BF16), "vw": mk(nc, "vw", [NTOK, 256], BF16),
           "kcT": mk(nc, "kcT", [4, 64, NCH * 128], BF16), "vcm": mk(nc, "vcm", [NCH * 128, 4, 64], BF16), "buf": Buf()}
    K = KB(nc)
    with contextlib.ExitStack() as es:
        load_consts(K, es, ident)
        bf = [Buf() for _ in range(NTF)]
        b3 = [Buf() for _ in range(NI)]
        b4 = [Buf() for _ in range(NI)]
        b5 = [Buf() for _ in range(NI)]
        b6 = [Buf() for _ in range(NI)]
        if stages[0] == "k":
            kv_stage(K, tiles_of(h3full, NTF), bf, kvd, kv_norm, kv_w, cmp_pe, cmp_w1, cmp_w2, k_norm)
        if stages[1] == "f":
            ffn_stage(K, tiles_of(h3own, NI), b3, tiles_of(h4, NI), b4, ffn_norm[0], ffn_w_in[0], ffn_w_out[0])
        if stages[2] == "n":
            nsa_stage(K, NTF, tiles_of(h4 if stages[1] == "f" else h3own, NI), b4, tiles_of(h5, NI), b5, kvd, mix_norm,
                      b_w_in, b_q_norm, b_w_out, cst)
        if stages[3] == "f":
            ffn_stage(K, tiles_of(h5, NI), b5, tiles_of(out, NI), b6, ffn_norm[1], ffn_w_in[1], ffn_w_out[1])
    K.S.close()
    return nc


def own_rows(a, p, NTF):
    t = a.reshape(NTF // 2, 2, 128, a.shape[-1])
    return np.ascontiguousarray(t[:, p].reshape(-1, a.shape[-1]))


def l1_inputs(inp, h3seq, p, NTF=32):
    d = {"h3full": np.ascontiguousarray(h3seq), "h3own": own_rows(h3seq, p, NTF), "ident": np.eye(128, dtype=np.float32),
         "ffn_norm": inp["ffn_norm"][1], "ffn_w_in": inp["ffn_w_in"][1], "ffn_w_out": inp["ffn_w_out"][1],
         "mix_norm": inp["mix_norm"][1], "kv_norm": inp["kv_norm"], "kv_w": inp["kv_w"], "cmp_pe": inp["cmp_pe"],
         "cmp_w1": inp["cmp_w1"], "cmp_w2": inp["cmp_w2"], "k_norm": inp["k_norm"], "b_w_in": inp["b_w_in"][0],
         "b_q_norm": inp["b_q_norm"][0], "b_w_out": inp["b_w_out"][0]}
    d.update(nsa_consts(NTF, p))
    return d


def build_fused(NTF=32, debug=False):
    NI = NTF // 2
    NTOK = NTF * 128
    NCH = max(1, NTF // 16)
    nc = bass.Bass("TRN2", target_bir_lowering=False)
    x = _din(nc, "x", [NTOK, 1024])
    ident = _din(nc, "ident", [128, 128])
    flag = _din(nc, "flag", [128, 2])
    ffn_norm = _din(nc, "ffn_norm", [2, 2, 1024])
    ffn_w_in = _din(nc, "ffn_w_in", [2, 2, 1024, 5632])
    ffn_w_out = _din(nc, "ffn_w_out", [2, 2, 2816, 1024])
    mix_norm = _din(nc, "mix_norm", [2, 1024])
    a_w_in = _din(nc, "a_w_in", [1024, 3088])
    a_b_if = _din(nc, "a_b_if", [2, 8])
    a_g_head = _din(nc, "a_g_head", [8, 128])
    a_w_out = _din(nc, "a_w_out", [1024, 1024])
    kv_norm = _din(nc, "kv_norm", [1024])
    kv_w = _din(nc, "kv_w", [1024, 1536])
    cmp_pe = _din(nc, "cmp_pe", [2, 32, 64])
    cmp_w1 = _din(nc, "cmp_w1", [2, 2048, 256])
    cmp_w2 = _din(nc, "cmp_w2", [2, 256, 64])
    k_norm = _din(nc, "k_norm", [3, 64])
    b_w_in = _din(nc, "b_w_in", [1024, 1072])
    b_q_norm = _din(nc, "b_q_norm", [64])
    b_w_out = _din(nc, "b_w_out", [1024, 1024])
    mcst = {k: _din(nc, k, v.shape) for k, v in mlstm_consts().items()}
    ncst = {k: _din(nc, k, v.shape) for k, v in nsa_consts(NTF, 0).items()}
    out = _dout(nc, "out", [NI * 128, 1024])
    mk = _dout if debug else _dint
    h1 = mk(nc, "h1", [NTOK, 1024])
    h2 = mk(nc, "h2", [NTOK, 1024])
    h3 = mk(nc, "h3", [NTOK, 1024])
    cf2 = _dint(nc, "cf2", [128, 516])
    h4 = mk(nc, "h4", [NI * 128, 1024])
    h5 = mk(nc, "h5", [NI * 128, 1024])
    kvd = {"ksT": _dint(nc, "ksT", [256, NTOK], BF16), "kwT": _dint(nc, "kwT", [256, NTOK], BF16),
           "vs": _dint(nc, "vs", [NTOK, 256], BF16), "vw": _dint(nc, "vw", [NTOK, 256], BF16),
           "kcT": _dint(nc, "kcT", [4, 64, NCH * 128], BF16), "vcm": _dint(nc, "vcm", [NCH * 128, 4, 64], BF16), "buf": Buf()}
    K = KB(nc)
    with contextlib.ExitStack() as es:
        load_consts(K, es, ident)
        bx = [Buf() for _ in range(NTF)]
        b1 = [Buf() for _ in range(NTF)]
        b2 = [Buf() for _ in range(NTF)]
        b3 = [Buf() for _ in range(NTF)]
        b4 = [Buf() for _ in range(NI)]
        b5 = [Buf() for _ in range(NI)]
        b6 = [Buf() for _ in range(NI)]
        tx, t1, t2, t3 = tiles_of(x, NTF), tiles_of(h1, NTF), tiles_of(h2, NTF), tiles_of(h3, NTF)
        for hf in range(2):
            sl = slice(hf * NI, (hf + 1) * NI)
            ffn_stage(K, tx[sl], bx[sl], t1[sl], b1[sl], ffn_norm[0, 0], ffn_w_in[0, 0], ffn_w_out[0, 0])
        mlstm_stage(K, t1, b1, t2, b2, mix_norm[0], a_w_in, a_b_if, a_g_head, a_w_out,
                    None, cf2.rearrange("p (m c) -> p m c", m=4), mcst)
        for hf in range(2):
            sl = slice(hf * NI, (hf + 1) * NI)
            ffn_stage(K, t2[sl], b2[sl], t3[sl], b3[sl], ffn_norm[0, 1], ffn_w_in[0, 1], ffn_w_out[0, 1])
        kv_stage(K, t3, b3, kvd, kv_norm, kv_w, cmp_pe, cmp_w1, cmp_w2, k_norm)
        ffn_stage(K, [t3[2 * i] for i in range(NI)], [b3[2 * i] for i in range(NI)], tiles_of(h4, NI), b4, ffn_norm[1, 0],
                  ffn_w_in[1, 0], ffn_w_out[1, 0], src2_tiles=[t3[2 * i + 1] for i in range(NI)],
                  src2_bufs=[b3[2 * i + 1] for i in range(NI)], flag_ap=flag)
        nsa_stage(K, NTF, tiles_of(h4, NI), b4, tiles_of(h5, NI), b5, kvd, mix_norm[1], b_w_in, b_q_norm, b_w_out, ncst)
        ffn_stage(K, tiles_of(h5, NI), b5, tiles_of(out, NI), b6, ffn_norm[1, 1], ffn_w_in[1, 1], ffn_w_out[1, 1])
    K.S.close()
    return nc


def fused_inputs(inp, xs, p, NTF=32):
    d = {"x": np.ascontiguousarray(xs), "ident": np.eye(128, dtype=np.float32),
         "flag": np.ascontiguousarray(np.broadcast_to(np.array([1.0 - p, float(p)], np.float32), (128, 2))),
         "ffn_norm": inp["ffn_norm"], "ffn_w_in": inp["ffn_w_in"], "ffn_w_out": inp["ffn_w_out"],
         "mix_norm": inp["mix_norm"], "a_w_in": inp["a_w_in"][0], "a_b_if": inp["a_b_if"][0],
         "a_g_head": inp["a_g_head"][0], "a_w_out": inp["a_w_out"][0], "kv_norm": inp["kv_norm"], "kv_w": inp["kv_w"],
         "cmp_pe": inp["cmp_pe"], "cmp_w1": inp["cmp_w1"], "cmp_w2": inp["cmp_w2"], "k_norm": inp["k_norm"],
         "b_w_in": inp["b_w_in"][0], "b_q_norm": inp["b_q_norm"][0], "b_w_out": inp["b_w_out"][0]}
    d.update(mlstm_consts())
    d.update(nsa_consts(NTF, p))
    return d


_CACHE = {}


def _get(name, fn):
    if name not in _CACHE:
        _CACHE[name] = fn()
    return _CACHE[name]


def kernel(**inputs):
    inp = {k: np.ascontiguousarray(np.asarray(v, dtype=np.float32)) for k, v in inputs.items()}
    x = inp["x"]
    B, T, D = x.shape
    ncore = 8
    nc = _get("fused", build_fused)
    ins = [fused_inputs(inp, x[c // 2], c % 2, 32) for c in range(ncore)]
    res = run_bass_kernel_spmd(nc, ins, core_ids=list(range(ncore))).results
    out = np.zeros((B, T // 256, 2, 128, D), np.float32)
    for c in range(ncore):
        out[c // 2, :, c % 2] = res[c]["out"].reshape(T // 256, 128, D)
    return out.reshape(B, T, D)
```
